# Optimizing a Trainium2 kernel written in Bass

```python
import math
import jax, jax.numpy as jnp
from jax import lax
import numpy as np

D_MODEL = 2048
BATCH = 2
SEQ = 16384
DEPTH = 2

RET_HEADS = 6
RET_QK_DIM = 64
RET_V_DIM = 128
RET_CHUNK = 128
ATT_HEADS = 6
ATT_HEAD_DIM = 128
IDX_HEADS = 4
IDX_DIM = 64
TOPK_MAX = 256
Q_BLOCK = 128
SG_GROUPS = 4
SG_GROUP_DIM = 128
SG_CHUNK = 128
REL_BUCKETS = 32
REL_MAX_DIST = 128
D_FF = 5632
ROPE_BASE = 10000.0
EPS = 1e-6

RET_QK_W = RET_HEADS * RET_QK_DIM
RET_V_W = RET_HEADS * RET_V_DIM
ATT_W = ATT_HEADS * ATT_HEAD_DIM
IDX_Q_W = IDX_HEADS * IDX_DIM
SG_W = SG_GROUPS * SG_GROUP_DIM
SPLIT_SIZES = (RET_QK_W, RET_QK_W, RET_V_W, RET_V_W, ATT_W, ATT_W, ATT_W,
               IDX_Q_W, IDX_DIM, IDX_HEADS, SG_W, SG_W, D_MODEL, D_MODEL, D_MODEL)
D_IN = sum(SPLIT_SIZES)
SPLIT_POINTS = tuple(int(p) for p in np.cumsum(SPLIT_SIZES)[:-1])

kernel_name = 'hybrid_retention_dsa_sgmlp_block'


def rmsnorm(x, g):
    xf = x.astype(jnp.float32)
    y = xf * lax.rsqrt(jnp.mean(xf * xf, axis=-1, keepdims=True) + EPS)
    return (y * g.astype(jnp.float32)).astype(x.dtype)


def head_groupnorm(o, g):
    B, S, H, dv = o.shape
    of = o.astype(jnp.float32)
    mu = jnp.mean(of, axis=-1, keepdims=True)
    var = jnp.mean(jnp.square(of - mu), axis=-1, keepdims=True)
    y = ((of - mu) * lax.rsqrt(var + EPS)).reshape(B, S, H * dv)
    return (y * g.astype(jnp.float32)).astype(o.dtype)


def swiglu(x, w_gate, w_up, w_down):
    return (jax.nn.silu(x @ w_gate) * (x @ w_up)) @ w_down


def rope(x, positions):
    d = x.shape[-1]
    inv_freq = ROPE_BASE ** (-jnp.arange(0, d, 2, dtype=jnp.float32) / d)
    ang = positions.astype(jnp.float32)[..., None] * inv_freq
    cos = jnp.cos(ang)[:, :, None, :]
    sin = jnp.sin(ang)[:, :, None, :]
    xf = x.astype(jnp.float32)
    x1, x2 = xf[..., : d // 2], xf[..., d // 2:]
    return jnp.concatenate([x1 * cos - x2 * sin, x1 * sin + x2 * cos], axis=-1).astype(x.dtype)


def retention(q, k, v):
    B, S, H, dk = q.shape
    dv = v.shape[-1]
    C = RET_CHUNK
    n = S // C
    dt = q.dtype
    log_g = jnp.log(1.0 - 2.0 ** (-5.0 - jnp.arange(H, dtype=jnp.float32)))
    idx = jnp.arange(C, dtype=jnp.float32)
    rel = idx[:, None] - idx[None, :]
    decay_intra = jnp.where(rel >= 0, jnp.exp(jnp.maximum(rel, 0.0) * log_g[:, None, None]), 0.0).astype(dt)
    k_decay = jnp.exp((C - 1 - idx)[None, :] * log_g[:, None]).astype(dt)
    q_decay = jnp.exp((idx + 1)[None, :] * log_g[:, None]).astype(dt)
    chunk_decay = jnp.exp(C * log_g).astype(dt)

    qc = q.reshape(B, n, C, H, dk) * (dk ** -0.5)
    kc = k.reshape(B, n, C, H, dk)
    vc = v.reshape(B, n, C, H, dv)
    scores = jnp.einsum('bnthd,bnshd->bnhts', qc, kc) * decay_intra
    o_intra = jnp.einsum('bnhts,bnshv->bnthv', scores, vc)
    chunk_kv = jnp.einsum('bnshd,hs,bnshv->bnhdv', kc, k_decay, vc)

    def step(state, kv):
        return state * chunk_decay[None, :, None, None] + kv, state

    init = jnp.zeros((B, H, dk, dv), chunk_kv.dtype)
    _, prev = lax.scan(step, init, jnp.moveaxis(chunk_kv, 1, 0))
    prev = jnp.moveaxis(prev, 0, 1)
    o_cross = jnp.einsum('bnthd,bnhdv,ht->bnthv', qc, prev, q_decay)
    return (o_intra + o_cross).reshape(B, S, H, dv)


def t5_bucket(dist):
    max_exact = REL_BUCKETS // 2
    dist = jnp.maximum(dist, 0)
    df = jnp.maximum(dist, 1).astype(jnp.float32)
    large = max_exact + (jnp.log(df / max_exact) / math.log(REL_MAX_DIST / max_exact)
                         * (REL_BUCKETS - max_exact)).astype(jnp.int32)
    large = jnp.minimum(large, REL_BUCKETS - 1)
    return jnp.where(dist < max_exact, dist, large)


def sparse_attention(q, k, v, q_idx, k_idx, w_idx, positions, rel_table):
    B, S, H, dh = q.shape
    n_keep = min(TOPK_MAX, S // 4)
    nb = S // Q_BLOCK
    key_pos = jnp.arange(S)
    w_scaled = w_idx * (IDX_HEADS ** -0.5)

    def block(bi):
        t0 = bi * Q_BLOCK
        qb = lax.dynamic_slice_in_dim(q, t0, Q_BLOCK, axis=1)
        qib = lax.dynamic_slice_in_dim(q_idx, t0, Q_BLOCK, axis=1)
        wib = lax.dynamic_slice_in_dim(w_scaled, t0, Q_BLOCK, axis=1)
        posb = lax.dynamic_slice_in_dim(positions, t0, Q_BLOCK, axis=1)
        tq = t0 + jnp.arange(Q_BLOCK)
        s_h = jax.nn.relu(jnp.einsum('bthd,bsd->bths', qib, k_idx).astype(jnp.float32) * (IDX_DIM ** -0.5))
        score = jnp.einsum('bths,bth->bts', s_h, wib.astype(jnp.float32))
        causal = key_pos[None, :] <= tq[:, None]
        score = jnp.where(causal[None], score, -jnp.inf)
        _, sel = lax.top_k(score, n_keep)
        k_sel = jax.vmap(lambda kb, ib: kb[ib])(k, sel)
        v_sel = jax.vmap(lambda vb, ib: vb[ib])(v, sel)
        pos_sel = jax.vmap(lambda pb, ib: pb[ib])(positions, sel)
        logits = jnp.einsum('bthd,btkhd->bthk', qb, k_sel).astype(jnp.float32) * (dh ** -0.5)
        bias = rel_table[t5_bucket(posb[:, :, None] - pos_sel)].astype(jnp.float32)
        logits = logits + jnp.transpose(bias, (0, 1, 3, 2))
        valid = sel <= tq[None, :, None]
        logits = jnp.where(valid[:, :, None, :], logits, -jnp.inf)
        p = jax.nn.softmax(logits, axis=-1).astype(v.dtype)
        return jnp.einsum('bthk,btkhd->bthd', p, v_sel)

    out = lax.map(block, jnp.arange(nb))
    return jnp.moveaxis(out, 0, 1).reshape(B, S, H * dh)


def spatial_gating(u, v, w_s, b_s):
    B, S, _ = u.shape
    n = S // SG_CHUNK
    vr = v.reshape(B, n, SG_CHUNK, SG_GROUPS, SG_GROUP_DIM)
    mask = jnp.tril(jnp.ones((SG_CHUNK, SG_CHUNK), w_s.dtype))
    mixed = jnp.einsum('gts,bnsgd->bntgd', w_s * mask, vr) + b_s.T[None, None, :, :, None]
    return u * mixed.reshape(B, S, SG_W)


def setup_inputs(seed: int = 0) -> dict:
    key = jax.random.key(seed)
    ks = jax.random.split(key, 24)
    f32 = jnp.float32

    def nrm(k, shape, fan_in, scale=1.0):
        return jax.random.normal(k, shape, f32) * (scale * fan_in ** -0.5)

    def gain(k, shape):
        return 1.0 + 0.02 * jax.random.normal(k, shape, f32)

    res_scale = (2.0 * DEPTH) ** -0.5
    return {
        'x': jax.random.normal(ks[0], (BATCH, SEQ, D_MODEL), f32),
        'positions': jnp.tile(jnp.arange(SEQ, dtype=jnp.int32)[None, :], (BATCH, 1)),
        'rel_table': 0.5 * jax.random.normal(ks[1], (REL_BUCKETS, ATT_HEADS), f32),
        'ffn1_norm': gain(ks[2], (DEPTH, D_MODEL)),
        'ffn1_w_gate': nrm(ks[3], (DEPTH, D_MODEL, D_FF), D_MODEL),
        'ffn1_w_up': nrm(ks[4], (DEPTH, D_MODEL, D_FF), D_MODEL),
        'ffn1_w_down': nrm(ks[5], (DEPTH, D_FF, D_MODEL), D_FF, res_scale),
        'mix_norm': gain(ks[6], (DEPTH, D_MODEL)),
        'w_in': nrm(ks[7], (DEPTH, D_MODEL, D_IN), D_MODEL),
        'ret_norm': gain(ks[8], (DEPTH, RET_V_W)),
        'sg_norm': gain(ks[9], (DEPTH, SG_W)),
        'sg_w': nrm(ks[10], (DEPTH, SG_GROUPS, SG_CHUNK, SG_CHUNK), SG_CHUNK, 0.5),
        'sg_b': 1.0 + 0.01 * jax.random.normal(ks[11], (DEPTH, SG_GROUPS, SG_CHUNK), f32),
        'w_br_ret': nrm(ks[12], (DEPTH, RET_V_W, D_MODEL), RET_V_W),
        'w_br_att': nrm(ks[13], (DEPTH, ATT_W, D_MODEL), ATT_W),
        'w_br_sg': nrm(ks[14], (DEPTH, SG_W, D_MODEL), SG_W),
        'w_out': nrm(ks[15], (DEPTH, D_MODEL, D_MODEL), D_MODEL, res_scale),
        'ffn2_norm': gain(ks[16], (DEPTH, D_MODEL)),
        'ffn2_w_gate': nrm(ks[17], (DEPTH, D_MODEL, D_FF), D_MODEL),
        'ffn2_w_up': nrm(ks[18], (DEPTH, D_MODEL, D_FF), D_MODEL),
        'ffn2_w_down': nrm(ks[19], (DEPTH, D_FF, D_MODEL), D_FF, res_scale),
        'final_norm': gain(ks[20], (D_MODEL,)),
    }


def reference(x, positions, rel_table, ffn1_norm, ffn1_w_gate, ffn1_w_up, ffn1_w_down,
              mix_norm, w_in, ret_norm, sg_norm, sg_w, sg_b, w_br_ret, w_br_att, w_br_sg,
              w_out, ffn2_norm, ffn2_w_gate, ffn2_w_up, ffn2_w_down, final_norm):
    B, S, _ = x.shape
    for l in range(DEPTH):
        x = x + 0.5 * swiglu(rmsnorm(x, ffn1_norm[l]), ffn1_w_gate[l], ffn1_w_up[l], ffn1_w_down[l])

        h = rmsnorm(x, mix_norm[l])
        proj = h @ w_in[l]
        (q_r, k_r, v_r, g_r, q_a, k_a, v_a, q_i, k_i, w_i,
         u_s, v_s, gate_r, gate_a, gate_s) = jnp.split(proj, SPLIT_POINTS, axis=-1)

        q_r = rope(q_r.reshape(B, S, RET_HEADS, RET_QK_DIM), positions)
        k_r = rope(k_r.reshape(B, S, RET_HEADS, RET_QK_DIM), positions)
        o_r = retention(q_r, k_r, v_r.reshape(B, S, RET_HEADS, RET_V_DIM))
        o_r = head_groupnorm(o_r, ret_norm[l]) * jax.nn.silu(g_r)

        o_a = sparse_attention(q_a.reshape(B, S, ATT_HEADS, ATT_HEAD_DIM),
                               k_a.reshape(B, S, ATT_HEADS, ATT_HEAD_DIM),
                               v_a.reshape(B, S, ATT_HEADS, ATT_HEAD_DIM),
                               q_i.reshape(B, S, IDX_HEADS, IDX_DIM), k_i, w_i,
                               positions, rel_table)

        u_s = jax.nn.gelu(u_s)
        v_s = rmsnorm(jax.nn.gelu(v_s), sg_norm[l])
        o_s = spatial_gating(u_s, v_s, sg_w[l], sg_b[l])

        merged = (jax.nn.sigmoid(gate_r) * (o_r @ w_br_ret[l])
                  + jax.nn.sigmoid(gate_a) * (o_a @ w_br_att[l])
                  + jax.nn.sigmoid(gate_s) * (o_s @ w_br_sg[l]))
        x = x + merged @ w_out[l]

        x = x + 0.5 * swiglu(rmsnorm(x, ffn2_norm[l]), ffn2_w_gate[l], ffn2_w_up[l], ffn2_w_down[l])
    return rmsnorm(x, final_norm)
```

```python
import math
import numpy as np
import ml_dtypes
import concourse.bass as bass
import concourse.mybir as mybir
from concourse.bass_utils import run_bass_kernel_spmd
from contextlib import ExitStack

F32 = mybir.dt.float32
BF16 = mybir.dt.bfloat16
I32 = mybir.dt.int32
AF = mybir.ActivationFunctionType
ALU = mybir.AluOpType
AX = mybir.AxisListType

D = 2048
DFF = 5632
DIN = 12100
NCH = 32
TT = 8
EPS = 1e-6


class Buf:
    __slots__ = ("name", "w", "r")

    def __init__(self, name=""):
        self.name = name
        self.w = None
        self.r = {}


def bufs(n, name=""):
    return [Buf(name + str(i)) for i in range(n)]


class Eng:
    def __init__(self, P, e, name, is_pe=False):
        self.e = e
        self.name = name
        self.is_pe = is_pe
        self.semkey = "c_" + name
        self.sem = P.new_sem(self.semkey)
        self.count = 0
        self.waited = {}
        self.dma_n = 0
        self.dma_sems = []


class Prog:
    NDMA = 8

    def __init__(self):
        self.nc = bass.Bass("TRN2", target_bir_lowering=False)
        self.es = ExitStack()
        self.sems = {}
        nc = self.nc
        self.pe = Eng(self, nc.tensor, "pe", is_pe=True)
        self.act = Eng(self, nc.scalar, "act")
        self.dve = Eng(self, nc.vector, "dve")
        self.pool = Eng(self, nc.gpsimd, "pool")
        self.sp = Eng(self, nc.sync, "sp")
        self.engs = [self.pe, self.act, self.dve, self.pool, self.sp]
        for q in (self.sp, self.pool):
            for i in range(self.NDMA):
                k = "d_%s_%d" % (q.name, i)
                self.new_sem(k)
                q.dma_sems.append(k)
        self.n_inst = 0
        self.rr = 0

    def new_sem(self, key):
        s = self.es.enter_context(self.nc.semaphore(key))
        self.sems[key] = s
        return s

    def sb(self, name, shape, dt, stack=None):
        self.uid = getattr(self, "uid", 0) + 1
        return (stack or self.es).enter_context(self.nc.sbuf_tensor("s%d_%s" % (self.uid, name), list(shape), dt))

    def ps(self, name, shape, dt, stack=None):
        return (stack or self.es).enter_context(self.nc.psum_tensor("p_" + name, list(shape), dt))

    def dram(self, name, shape, dt, kind="Internal"):
        return self.nc.dram_tensor(name, list(shape), dt, kind=kind).ap()

    def _deps(self, eng, reads, writes):
        deps = {}

        def add(tok, same_ok):
            if tok is None:
                return
            k, v = tok
            if k == eng.semkey and same_ok:
                return
            if deps.get(k, 0) < v:
                deps[k] = v
        for b in reads:
            add(b.w, eng.is_pe)
        for b in writes:
            add(b.w, True)
            for k, v in b.r.items():
                add((k, v), True)
        return deps

    def _wait(self, eng, deps):
        for k, v in deps.items():
            if eng.waited.get(k, 0) < v:
                eng.e.wait_ge(self.sems[k], v)
                eng.waited[k] = v

    def _commit(self, tok, reads, writes):
        k, v = tok
        for b in reads:
            if b.r.get(k, 0) < v:
                b.r[k] = v
        for b in writes:
            b.w = tok
            b.r = {}

    def op(self, eng, fn, *args, reads=(), writes=(), inc=True, **kw):
        self._wait(eng, self._deps(eng, reads, writes))
        inst = fn(*args, **kw)
        self.n_inst += 1
        if inc:
            eng.count += 1
            inst.then_inc(eng.sem, 1)
            tok = (eng.semkey, eng.count)
        else:
            tok = (eng.semkey, eng.count + 1)
        self._commit(tok, reads, writes)
        return inst

    def dma(self, q, out, in_, reads=(), writes=(), **kw):
        deps = self._deps(q, reads, writes)
        i = q.dma_n % self.NDMA
        rnd = q.dma_n // self.NDMA
        q.dma_n += 1
        k = q.dma_sems[i]
        if rnd > 0 and deps.get(k, 0) < 16 * rnd:
            deps[k] = 16 * rnd
        self._wait(q, deps)
        q.e.dma_start(out=out, in_=in_, **kw).then_inc(self.sems[k], 16)
        self.n_inst += 1
        self._commit((k, 16 * (rnd + 1)), reads, writes)

    def barrier(self):
        targets = {}
        for e in self.engs:
            if e.count > 0:
                targets[e.semkey] = e.count
            for i, k in enumerate(e.dma_sems):
                n = (e.dma_n - i + self.NDMA - 1) // self.NDMA
                if n > 0:
                    targets[k] = 16 * n
        for e in self.engs:
            for k, v in targets.items():
                if e.waited.get(k, 0) < v:
                    e.e.wait_ge(self.sems[k], v)
                    e.waited[k] = v

    def ev(self):
        self.rr += 1
        return self.act if self.rr & 1 else self.dve

    def copy(self, eng, out, in_, reads, writes):
        if eng is self.act:
            return self.op(eng, eng.e.activation, out=out, in_=in_, func=AF.Copy, reads=reads, writes=writes)
        return self.op(eng, eng.e.tensor_copy, out, in_, reads=reads, writes=writes)


class Common:
    def __init__(self, P):
        self.P = P
        nc = P.nc
        self.mm = [P.ps("mm%d" % i, [128, 512], F32) for i in range(6)]
        self.mmb = bufs(6, "mm")
        self.tp = [P.ps("tp%d" % i, [128, 1024], BF16) for i in range(2)]
        self.tpb = bufs(2, "tp")
        self.mi = 0
        self.ti = 0
        self.ai = 0
        self.mm_n = 6
        self.ident_f = P.sb("ident_f", [128, 128], F32)
        self.ident = P.sb("ident", [128, 128], BF16)
        self.identb = Buf("ident")
        P.op(P.pool, nc.gpsimd.memset, self.ident_f[:], 1.0, writes=[self.identb])
        P.op(P.pool, nc.gpsimd.affine_select, out=self.ident_f[:], in_=self.ident_f[:], pattern=[[1, 128]],
             compare_op=ALU.is_equal, fill=0.0, base=0, channel_multiplier=-1,
             reads=[self.identb], writes=[self.identb])
        P.op(P.pool, nc.gpsimd.tensor_copy, self.ident[:], self.ident_f[:], reads=[self.identb], writes=[self.identb])
        self.NS = 4
        self.si = 0

    def alloc_slots(self, stack):
        self.slot = [self.P.sb("wslot%d_%d" % (i, self.si), [128, 8192], BF16, stack) for i in range(self.NS)]
        self.slotb = bufs(self.NS, "slot")

    def bank(self):
        i = self.mi % self.mm_n
        self.mi += 1
        return self.mm[i], self.mmb[i]

    def accbank(self):
        i = 4 + (self.ai % 2)
        self.ai += 1
        return self.mm[i], self.mmb[i]

    def tbank(self):
        i = self.ti % 2
        self.ti += 1
        return self.tp[i], self.tpb[i]

    def load_w(self, w_ap, k0, kn, c0, cn):
        P = self.P
        i = self.si % self.NS
        self.si += 1
        view = self.slot[i][:, 0:kn * cn].rearrange("p (k n) -> p k n", k=kn)
        src = w_ap[k0 * 128:(k0 + kn) * 128, c0:c0 + cn].rearrange("(k p) n -> p k n", p=128)
        P.dma(P.pool, view, src, writes=[self.slotb[i]])
        return view, self.slotb[i]


def transpose_to(P, C, src_fn, nk, dst_fn, rbufs, wbufs, dtype_ok=True):
    nc = P.nc
    k = 0
    while k < nk:
        kn = min(8, nk - k)
        tp, tpb = C.tbank()
        for kk in range(kn):
            P.op(P.pe, nc.tensor.transpose, tp[:, kk * 128:(kk + 1) * 128], src_fn(k + kk), C.ident[:],
                 reads=list(rbufs) + [C.identb], writes=[tpb], inc=(kk == kn - 1))
        e = P.ev()
        P.copy(e, dst_fn(k, kn), tp[:, 0:kn * 128].rearrange("p (k n) -> p k n", k=kn), reads=[tpb], writes=wbufs)
        k += kn


class Work:
    def __init__(self, P, C, stack):
        self.P = P
        self.C = C
        _orig = P.sb
        Work.n = getattr(Work, "n", 0) + 1
        sfx = "_w%d" % Work.n
        P_sb = lambda name, shape, dt, st: _orig(name + sfx, shape, dt, st)
        self._init(P_sb, stack)

    def _init(self, sb, stack):
        P = type("X", (), {"sb": staticmethod(sb)})
        self.xt = P.sb("xt", [128, 4, D], F32, stack)
        self.xtb = bufs(4, "xt")
        self.xT = P.sb("xT", [128, 16, 512], BF16, stack)
        self.xTb = Buf("xT")
        self.hT = P.sb("hT", [128, 44, 512], BF16, stack)
        self.hTb = bufs(44, "hT")
        self.gbc = P.sb("gbc", [128, D], F32, stack)
        self.gbcb = Buf("gbc")
        self.xn = P.sb("xn", [128, D], BF16, stack)
        self.xnb = Buf("xn")
        self.junk = P.sb("junk", [128, D], BF16, stack)
        self.junkb = Buf("junk")
        self.st = P.sb("stats", [128, 16], F32, stack)
        self.stb = Buf("stats")
        self.sg = [P.sb("sgt%d" % i, [128, 512], BF16, stack) for i in range(2)]
        self.sgb = bufs(2, "sgt")
        self.sgi = 0


def norm_T(P, C, W, g_ap):
    nc = P.nc
    P.dma(P.sp, W.gbc[:], g_ap.partition_broadcast(128), writes=[W.gbcb])
    for c in range(4):
        ss = W.st[:, c:c + 1]
        sd = W.st[:, 4 + c:5 + c]
        rs = W.st[:, 8 + c:9 + c]
        P.op(P.act, nc.scalar.activation, out=W.junk[:], in_=W.xt[:, c, :], func=AF.Square, accum_out=ss,
             reads=[W.xtb[c]], writes=[W.junkb, W.stb])
        P.op(P.act, nc.scalar.activation, out=sd, in_=ss, func=AF.Sqrt, scale=1.0 / D, bias=C.epsb[:, 0:1],
             reads=[W.stb], writes=[W.stb])
        P.op(P.dve, nc.vector.reciprocal, rs, sd, reads=[W.stb], writes=[W.stb])
        P.op(P.dve, nc.vector.scalar_tensor_tensor, out=W.xn[:], in0=W.xt[:, c, :], scalar=rs, in1=W.gbc[:],
             op0=ALU.mult, op1=ALU.mult, reads=[W.xtb[c], W.stb, W.gbcb], writes=[W.xnb])
        transpose_to(P, C, lambda k: W.xn[:, k * 128:(k + 1) * 128], 16,
                     lambda k0, kn: W.xT[:, k0:k0 + kn, c * 128:(c + 1) * 128], [W.xnb], [W.xTb])


def linear_tok(P, C, actT_fn, abufs, K, w_ap, c0, cn, evac, nchunks=4, kg_max=None):
    nc = P.nc
    kg = min(K, 8192 // cn)
    if kg_max:
        kg = min(kg, kg_max)
    banks = [C.bank() for _ in range(nchunks)]
    k0 = 0
    while k0 < K:
        kn = min(kg, K - k0)
        wv, wb = C.load_w(w_ap, k0, kn, c0, cn)
        for c in range(nchunks):
            ps, pb = banks[c]
            for kk in range(kn):
                k = k0 + kk
                P.op(P.pe, nc.tensor.matmul, ps[:, 0:cn], actT_fn(k, c), wv[:, kk, :], start=(k == 0), stop=(k == K - 1),
                     reads=list(abufs) + [wb], writes=[pb], inc=(kk == kn - 1))
        k0 += kn
    for c in range(nchunks):
        ps, pb = banks[c]
        evac(c, ps[:, 0:cn], pb)


def ffn(P, C, W, g_ap, wg, wu, wd):
    nc = P.nc
    norm_T(P, C, W, g_ap)
    for fg in range(11):
        gv, gb = C.load_w(wg, 0, 16, fg * 512, 512)
        uv, ub = C.load_w(wu, 0, 16, fg * 512, 512)
        for fc in range(4):
            f = fg * 4 + fc
            pg, pgb = C.bank()
            pu, pub = C.bank()
            for k in range(16):
                P.op(P.pe, nc.tensor.matmul, pg[:], gv[:, k, fc * 128:(fc + 1) * 128], W.xT[:, k, :], start=(k == 0), stop=(k == 15),
                     reads=[W.xTb, gb], writes=[pgb], inc=(k == 15))
            for k in range(16):
                P.op(P.pe, nc.tensor.matmul, pu[:], uv[:, k, fc * 128:(fc + 1) * 128], W.xT[:, k, :], start=(k == 0), stop=(k == 15),
                     reads=[W.xTb, ub], writes=[pub], inc=(k == 15))
            si = W.sgi % 2
            W.sgi += 1
            P.op(P.act, nc.scalar.activation, out=W.sg[si][:], in_=pg[:], func=AF.Silu, reads=[pgb], writes=[W.sgb[si]])
            P.op(P.dve, nc.vector.tensor_tensor, W.hT[:, f, :], W.sg[si][:], pu[:], ALU.mult,
                 reads=[W.sgb[si], pub], writes=[W.hTb[f]])
    for nb in range(4):
        def evac(c, ps, pb, nb=nb):
            xs = W.xt[:, c, nb * 512:(nb + 1) * 512]
            P.op(P.dve, nc.vector.scalar_tensor_tensor, out=xs, in0=ps, scalar=0.5, in1=xs, op0=ALU.mult, op1=ALU.add,
                 reads=[pb, W.xtb[c]], writes=[W.xtb[c]])
        linear_tok(P, C, lambda k, c: W.hT[:, k, c * 128:(c + 1) * 128], W.hTb, 44, wd, nb * 512, 512, evac, kg_max=11)


def add_consts(P, C):
    nc = P.nc
    C.epsb = P.sb("epsb", [128, 1], F32)
    C.epsbb = Buf("eps")
    P.op(P.pool, nc.gpsimd.memset, C.epsb[:], EPS, writes=[C.epsbb])


def build_test_ffn():
    P = Prog()
    C = Common(P)
    add_consts(P, C)
    nc = P.nc
    xin = P.dram("xin", [512, D], F32, "ExternalInput")
    g = P.dram("g", [D], F32, "ExternalInput")
    wg = P.dram("wg", [D, DFF], F32, "ExternalInput")
    wu = P.dram("wu", [D, DFF], F32, "ExternalInput")
    wd = P.dram("wd", [DFF, D], F32, "ExternalInput")
    y = P.dram("y", [512, D], F32, "ExternalOutput")
    W = Work(P, C, P.es)
    C.alloc_slots(P.es)
    P.dma(P.sp, W.xt[:], xin.rearrange("(c p) d -> p c d", p=128), writes=W.xtb)
    ffn(P, C, W, g, wg, wu, wd)
    P.dma(P.sp, y.rearrange("(c p) d -> p c d", p=128), W.xt[:], reads=W.xtb)
    P.barrier()
    return P


C_QK, C_V, C_G, C_QA, C_KA, C_VA, C_QI, C_US, C_GATE = 0, 768, 1536, 2304, 3072, 3840, 4608, 4932, 5956
TWO_PI = 2.0 * math.pi
CW1 = 6.28125
CW2 = TWO_PI - CW1


def region(W, fa, fb, dt, shape_str=None, **kw):
    ap = W.hT[:, fa:fb, :].rearrange("p a b -> p (a b)")
    if dt != BF16:
        ap = ap.bitcast(dt)
    if shape_str:
        ap = ap.rearrange(shape_str, **kw)
    return ap, W.hTb[fa:fb]


def trig_tables(P, C, pos_ap, invf_ap, stack):
    nc = P.nc
    T = {}
    T["sin"] = P.sb("trig_sin", [128, NCH, 32], F32, stack)
    T["cos"] = P.sb("trig_cos", [128, NCH, 32], F32, stack)
    stack = ExitStack()
    posi = P.sb("posi", [128, NCH], I32, stack)
    posf = P.sb("posf", [128, NCH], F32, stack)
    invf = P.sb("invf", [128, 32], F32, stack)
    b = Buf("trig")
    P.dma(P.sp, posi[:], pos_ap, writes=[b])
    P.dma(P.sp, invf[:], invf_ap, writes=[b])
    P.op(P.dve, nc.vector.tensor_copy, posf[:], posi[:], reads=[b], writes=[b])
    ang = P.sb("ang", [128, NCH, 32], F32, stack)
    a = P.sb("anga", [128, NCH * 32], F32, stack)
    kf = P.sb("angk", [128, NCH * 32], F32, stack)
    ki = P.sb("angki", [128, NCH * 32], I32, stack)
    m = P.sb("angm", [128, NCH * 32], F32, stack)
    P.op(P.dve, nc.vector.tensor_tensor, ang[:], posf[:].unsqueeze(2).to_broadcast([128, NCH, 32]),
         invf[:].unsqueeze(1).to_broadcast([128, NCH, 32]), ALU.mult, reads=[b], writes=[b])
    angf = ang[:].rearrange("p a b -> p (a b)")
    for name, shift in (("sin", 0.0), ("cos", math.pi / 2)):
        out = T[name]
        o2 = out[:].rearrange("p a b -> p (a b)")
        V = nc.vector
        P.op(P.dve, V.tensor_scalar, a[:], angf, shift, None, ALU.add, reads=[b], writes=[b])
        P.op(P.dve, V.tensor_scalar, kf[:], a[:], 1.0 / TWO_PI, None, ALU.mult, reads=[b], writes=[b])
        P.op(P.dve, V.tensor_copy, ki[:], kf[:], reads=[b], writes=[b])
        P.op(P.dve, V.tensor_copy, kf[:], ki[:], reads=[b], writes=[b])
        P.op(P.dve, V.scalar_tensor_tensor, out=a[:], in0=kf[:], scalar=-CW1, in1=a[:], op0=ALU.mult, op1=ALU.add, reads=[b], writes=[b])
        P.op(P.dve, V.scalar_tensor_tensor, out=a[:], in0=kf[:], scalar=-CW2, in1=a[:], op0=ALU.mult, op1=ALU.add, reads=[b], writes=[b])
        P.op(P.dve, V.tensor_scalar, m[:], a[:], math.pi, -TWO_PI, ALU.is_gt, ALU.mult, reads=[b], writes=[b])
        P.op(P.dve, V.tensor_tensor, a[:], a[:], m[:], ALU.add, reads=[b], writes=[b])
        P.op(P.dve, V.tensor_scalar, m[:], a[:], -math.pi, TWO_PI, ALU.is_lt, ALU.mult, reads=[b], writes=[b])
        P.op(P.dve, V.tensor_tensor, a[:], a[:], m[:], ALU.add, reads=[b], writes=[b])
        P.op(P.dve, V.tensor_scalar, a[:], a[:], math.pi, -math.pi, ALU.min, ALU.max, reads=[b], writes=[b])
        P.op(P.act, nc.scalar.activation, out=o2, in_=a[:], func=AF.Sin, reads=[b], writes=[b])
    T["b"] = b
    P.barrier()
    stack.close()
    return T


def phase_a(P, C, l, xin, io, first):
    nc = P.nc
    V = nc.vector
    with ExitStack() as st:
        W = Work(P, C, st)
        C.alloc_slots(st)
        TR = trig_tables(P, C, io["pos"], io["invf"], st)
        cb = Buf("constsA")
        dec = P.sb("dec", [128, 12], F32, st)
        P.dma(P.sp, dec[:], io["dec"], writes=[cb])
        sgn_bc = P.sb("sgn_bc", [128, 512], F32, st)
        P.dma(P.sp, sgn_bc[:], io["sg_norm"].partition_broadcast(128), writes=[cb])
        wsT_f = P.sb("wsT_f", [128, 4, 128], F32, st)
        wsT = P.sb("wsT", [128, 4, 128], BF16, st)
        P.dma(P.sp, wsT_f[:], io["sg_wT"], writes=[cb])
        for g in range(4):
            P.op(P.pool, nc.gpsimd.affine_select, out=wsT_f[:, g, :], in_=wsT_f[:, g, :], pattern=[[1, 128]],
                 compare_op=ALU.is_ge, fill=0.0, base=0, channel_multiplier=-1, reads=[cb], writes=[cb])
        P.op(P.pool, nc.gpsimd.tensor_copy, wsT[:], wsT_f[:], reads=[cb], writes=[cb])
        sgb = P.sb("sgb", [128, 4], F32, st)
        P.dma(P.sp, sgb[:], io["sg_bT"], writes=[cb])

        win = io["w_in"]
        for tt in range(TT):
            rows = slice(tt * 512, (tt + 1) * 512)
            P.dma(P.sp, W.xt[:], xin[rows, :].rearrange("(c p) d -> p c d", p=128), writes=W.xtb)
            ffn(P, C, W, io["n1"], io["wg"], io["wu"], io["wd"])
            P.dma(P.sp, io["x1"][rows, :].rearrange("(c p) d -> p c d", p=128), W.xt[:], reads=W.xtb)
            norm_T(P, C, W, io["nmix"])
            act = lambda k, c: W.xT[:, k, c * 128:(c + 1) * 128]

            stg, stgb = region(W, 0, 16, F32, "p (c n) -> p c n", c=4)
            obf, obfb = region(W, 16, 28, BF16, "p (c n) -> p c n", c=4)
            tmp, tmpb = region(W, 28, 36, F32)
            vnb, vnbb = region(W, 36, 40, BF16)
            kvs, kvsb = region(W, 40, 43, F32)

            def to_stg(col0, width, off=0):
                n0 = 0
                while n0 < width:
                    nn = min(512, width - n0)

                    def evac(c, ps, pb, n0=n0, nn=nn):
                        P.copy(P.ev(), stg[:, c, off + n0:off + n0 + nn], ps, reads=[pb], writes=stgb)
                    linear_tok(P, C, act, [W.xTb], 16, win, col0 + n0, nn, evac)
                    n0 += nn

            def simple(col0, width, func, scale, out_ap, ooff=0):
                n0 = 0
                while n0 < width:
                    nn = min(512, width - n0)

                    def evac(c, ps, pb, n0=n0, nn=nn):
                        P.op(P.act, nc.scalar.activation, out=obf[:, c, n0:n0 + nn], in_=ps, func=func, scale=scale,
                             reads=[pb], writes=obfb)
                    linear_tok(P, C, act, [W.xTb], 16, win, col0 + n0, nn, evac)
                    n0 += nn
                P.dma(P.sp, out_ap[rows, ooff:ooff + width].rearrange("(c p) n -> p c n", p=128), obf[:, :, 0:width], reads=obfb)

            to_stg(C_QK, 768)
            ta = tmp[:, 0:384].rearrange("p (h j) -> p h j", h=12)
            tb = tmp[:, 384:768].rearrange("p (h j) -> p h j", h=12)
            rot = tmp[:, 768:1536].rearrange("p (h j) -> p h j", h=12)
            for c in range(4):
                i = tt * 4 + c
                x = stg[:, c, 0:768].rearrange("p (h j) -> p h j", h=12)
                x1, x2 = x[:, :, 0:32], x[:, :, 32:64]
                cs = TR["cos"][:, i, :].unsqueeze(1).to_broadcast([128, 12, 32])
                sn = TR["sin"][:, i, :].unsqueeze(1).to_broadcast([128, 12, 32])
                rw = dict(reads=stgb + tmpb + [TR["b"]], writes=tmpb)
                P.op(P.dve, V.tensor_tensor, ta, x1, cs, ALU.mult, **rw)
                P.op(P.dve, V.tensor_tensor, tb, x2, sn, ALU.mult, **rw)
                P.op(P.dve, V.tensor_tensor, rot[:, :, 0:32], ta, tb, ALU.subtract, **rw)
                P.op(P.dve, V.tensor_tensor, ta, x1, sn, ALU.mult, **rw)
                P.op(P.dve, V.tensor_tensor, tb, x2, cs, ALU.mult, **rw)
                P.op(P.dve, V.tensor_tensor, rot[:, :, 32:64], ta, tb, ALU.add, **rw)
                o = obf[:, c, :].rearrange("p (a h j) -> p a h j", a=4, h=6)
                rq, rk = rot[:, 0:6, :], rot[:, 6:12, :]
                qd_bc = dec[:, 0:6].unsqueeze(2).to_broadcast([128, 6, 64])
                kd_bc = dec[:, 6:12].unsqueeze(2).to_broadcast([128, 6, 64])
                ww = dict(reads=tmpb + [cb], writes=obfb)
                P.op(P.dve, V.tensor_scalar, o[:, 0], rq, 0.125, None, ALU.mult, **ww)
                P.op(P.dve, V.scalar_tensor_tensor, out=o[:, 1], in0=rq, scalar=0.125, in1=qd_bc, op0=ALU.mult, op1=ALU.mult, **ww)
                P.op(P.dve, V.tensor_copy, o[:, 2], rk, **ww)
                P.op(P.dve, V.tensor_tensor, o[:, 3], rk, kd_bc, ALU.mult, **ww)
            P.dma(P.sp, io["qkr"][rows, :].rearrange("(c p) n -> p c n", p=128), obf[:, :, :], reads=obfb)
            kdk = vnb[:, 0:1536].rearrange("p (c n) -> p c n", c=4)
            P.op(P.dve, V.tensor_copy, kdk, obf[:, :, 1152:1536], reads=obfb, writes=vnbb)

            simple(C_V, 768, AF.Copy, 1.0, io["vr"])
            for c in range(4):
                i = tt * 4 + c
                pA, pAb = C.bank()
                pB, pBb = C.bank()
                for h in range(6):
                    ps, pb = (pA, pAb) if h < 4 else (pB, pBb)
                    hh = h if h < 4 else h - 4
                    P.op(P.pe, nc.tensor.matmul, ps[0:64, hh * 128:(hh + 1) * 128], kdk[:, c, h * 64:(h + 1) * 64],
                         obf[:, c, h * 128:(h + 1) * 128], start=True, stop=True, reads=vnbb + obfb, writes=[pb])
                P.copy(P.dve, kvs[0:64, 0:512], pA[0:64, :], reads=[pAb], writes=kvsb)
                P.copy(P.act, kvs[0:64, 512:768], pB[0:64, 0:256], reads=[pBb], writes=kvsb)
                P.dma(P.sp, io["kv"][i].rearrange("h d v -> d h v"), kvs[0:64, :].rearrange("d (h v) -> d h v", h=6), reads=kvsb)
            simple(C_G, 768, AF.Silu, 1.0, io["sgr"])
            simple(C_QA, 768, AF.Copy, 128 ** -0.5, io["qa"])
            simple(C_KA, 768, AF.Copy, 1.0, io["ka"])
            simple(C_VA, 768, AF.Copy, 1.0, io["va"])
            to_stg(C_QI, 324)
            sg4, sg4b = kvs[:, 0:16].rearrange("p (c n) -> p c n", c=4), kvsb
            aw = kvs[:, 16:32].rearrange("p (c n) -> p c n", c=4)
            P.op(P.act, nc.scalar.activation, out=aw, in_=stg[:, :, 320:324], func=AF.Abs, scale=0.0625, reads=stgb, writes=kvsb)
            P.op(P.act, nc.scalar.activation, out=sg4, in_=stg[:, :, 320:324], func=AF.Sign, reads=stgb, writes=kvsb)
            for c in range(4):
                P.op(P.dve, V.tensor_tensor, obf[:, c, 0:256].rearrange("p (h j) -> p h j", h=4),
                     stg[:, c, 0:256].rearrange("p (h j) -> p h j", h=4), aw[:, c, :].unsqueeze(2).to_broadcast([128, 4, 64]),
                     ALU.mult, reads=stgb + kvsb, writes=obfb)
            P.op(P.dve, V.tensor_copy, obf[:, :, 256:320], stg[:, :, 256:320], reads=stgb, writes=obfb)
            P.dma(P.sp, io["qwki"][rows, :].rearrange("(c p) n -> p c n", p=128), obf[:, :, 0:320], reads=obfb)
            P.dma(P.sp, io["sgn"][rows, :].rearrange("(c p) n -> p c n", p=128), sg4, reads=kvsb)
            to_stg(C_US, 1024)
            t2 = tmp[:, 0:1024]
            for c in range(4):
                x = stg[:, c, :]
                rw = dict(reads=stgb + tmpb, writes=tmpb)
                P.op(P.act, nc.scalar.activation, out=t2, in_=x, func=AF.Square, **rw)
                P.op(P.dve, V.tensor_scalar, t2, t2, 0.044715, 1.0, ALU.mult, ALU.add, **rw)
                P.op(P.dve, V.tensor_tensor, t2, t2, x, ALU.mult, **rw)
                P.op(P.act, nc.scalar.activation, out=t2, in_=t2, func=AF.Sigmoid, scale=1.5957691216057308, **rw)
                P.op(P.dve, V.tensor_tensor, x, x, t2, ALU.mult, reads=stgb + tmpb, writes=stgb)
                ss, sd, rs = tmp[:, 1024:1025], tmp[:, 1025:1026], tmp[:, 1026:1027]
                P.op(P.act, nc.scalar.activation, out=t2[:, 0:512], in_=x[:, 512:1024], func=AF.Square, accum_out=ss, **rw)
                P.op(P.act, nc.scalar.activation, out=sd, in_=ss, func=AF.Sqrt, scale=1.0 / 512, bias=C.epsb[:, 0:1], **rw)
                P.op(P.dve, V.reciprocal, rs, sd, **rw)
                vn = vnb[:, 1536:2048]
                P.op(P.dve, V.scalar_tensor_tensor, out=vn, in0=x[:, 512:1024], scalar=rs, in1=sgn_bc[:], op0=ALU.mult, op1=ALU.mult,
                     reads=stgb + tmpb + [cb], writes=vnbb)
                ps, pb = C.bank()
                for g in range(4):
                    P.op(P.pe, nc.tensor.matmul, ps[:, g * 128:(g + 1) * 128], wsT[:, g, :], vn[:, g * 128:(g + 1) * 128],
                         start=True, stop=True, reads=vnbb + [cb], writes=[pb])
                for g in range(4):
                    P.op(P.dve, V.scalar_tensor_tensor, out=obf[:, c, g * 128:(g + 1) * 128], in0=ps[:, g * 128:(g + 1) * 128],
                         scalar=sgb[:, g:g + 1], in1=x[:, g * 128:(g + 1) * 128], op0=ALU.add, op1=ALU.mult,
                         reads=[pb, cb] + stgb, writes=obfb)
            P.dma(P.sp, io["os"][rows, :].rearrange("(c p) n -> p c n", p=128), obf[:, :, 0:512], reads=obfb)
            for gi in range(8):
                simple(C_GATE + gi * 768, 768, AF.Sigmoid, 1.0, io["gates"], ooff=gi * 768)
        P.barrier()


A_OUTS = [("x1", [4096, D], F32), ("qkr", [4096, 1536], BF16), ("vr", [4096, 768], BF16), ("sgr", [4096, 768], BF16),
          ("qa", [4096, 768], BF16), ("ka", [4096, 768], BF16), ("va", [4096, 768], BF16), ("qwki", [4096, 320], BF16),
          ("sgn", [4096, 4], F32), ("os", [4096, 512], BF16), ("gates", [4096, 6144], BF16), ("kv", [NCH, 6, 64, 128], F32)]
A_INS = [("pos", [128, NCH], I32), ("invf", [128, 32], F32), ("dec", [128, 12], F32), ("sg_norm", [512], F32),
         ("sg_wT", [128, 4, 128], F32), ("sg_bT", [128, 4], F32), ("n1", [D], F32), ("wg", [D, DFF], F32),
         ("wu", [D, DFF], F32), ("wd", [DFF, D], F32), ("nmix", [D], F32), ("w_in", [D, DIN], F32)]


def declare(P, specs, kind, sfx=""):
    return {n: P.dram(n + sfx, s, dt, kind) for n, s, dt in specs}


def build_a():
    P = Prog()
    C = Common(P)
    add_consts(P, C)
    io = declare(P, A_INS, "ExternalInput")
    io.update(declare(P, A_OUTS, "ExternalOutput"))
    xin = P.dram("xin", [4096, D], F32, "ExternalInput")
    phase_a(P, C, 0, xin, io, True)
    return P


def ret_consts():
    h = np.arange(6, dtype=np.float64)
    log_g = np.log(1.0 - 2.0 ** (-5.0 - h))
    idx = np.arange(128, dtype=np.float64)
    kdec = np.exp((127 - idx)[:, None] * log_g[None, :])
    qdec = np.exp((idx + 1)[:, None] * log_g[None, :])
    cd = np.exp(128 * log_g)
    rel = idx[:, None] - idx[None, :]
    dintra = np.where(rel >= 0, np.exp(np.maximum(rel, 0)[None] * log_g[:, None, None]), 0.0)
    return dict(kdec=kdec.astype(np.float32), qdec=qdec.astype(np.float32), cd=cd, dintra=dintra.astype(np.float32))


def own_rows(a, j):
    s = a.shape
    return np.ascontiguousarray(a.reshape((128, 128) + s[1:])[j::4].reshape((4096,) + s[1:]))


def a_inputs(inp, l, c, xcur):
    b, j = c // 4, c % 4
    rc = ret_consts()
    invf = (10000.0 ** (-np.arange(0, 64, 2, dtype=np.float32) / 64)).astype(np.float32)
    m = {
        "xin": xcur[c],
        "pos": np.ascontiguousarray(inp["positions"][b].reshape(128, 128)[j::4].T.astype(np.int32)),
        "invf": np.ascontiguousarray(np.broadcast_to(invf[None, :], (128, 32))),
        "dec": np.ascontiguousarray(np.concatenate([rc["qdec"], rc["kdec"]], axis=1)),
        "sg_norm": inp["sg_norm"][l],
        "sg_wT": np.ascontiguousarray(inp["sg_w"][l].transpose(2, 0, 1)),
        "sg_bT": np.ascontiguousarray(inp["sg_b"][l].T),
        "n1": inp["ffn1_norm"][l], "wg": inp["ffn1_w_gate"][l], "wu": inp["ffn1_w_up"][l], "wd": inp["ffn1_w_down"][l],
        "nmix": inp["mix_norm"][l], "w_in": inp["w_in"][l],
    }
    return m


NBIS = 14
NEG = -1.0e30


def phase_b1(P, C, l, io, nch=NCH):
    nc = P.nc
    V = nc.vector
    C.mm_n = 4
    with ExitStack() as st:
        cb = Buf("constsB")
        dintra = P.sb("dintra", [128, 768], F32, st)
        P.dma(P.sp, dintra[:], io["dintraT"], writes=[cb])
        cA = P.sb("cA", [64, 5, 6], F32, st)
        cU = P.sb("cU", [64, 5, 6], F32, st)
        P.dma(P.sp, cA[:], io["cA"], writes=[cb])
        P.dma(P.sp, cU[:], io["cU"], writes=[cb])
        gret = P.sb("gret", [128, 768], F32, st)
        P.dma(P.sp, gret[:], io["ret_norm"].partition_broadcast(128), writes=[cb])
        cmask = P.sb("cmask", [128, 512], F32, st)
        P.dma(P.sp, cmask[:], io["cmask"], writes=[cb])
        nearb = P.sb("nearb", [128, 6, 8, 128], BF16, st)
        P.dma(P.sp, nearb[:], io["nearbias"], writes=[cb])
        farb = P.sb("farb", [128, 6], F32, st)
        P.dma(P.sp, farb[:], io["farbias"], writes=[cb])
        sgn = P.sb("sgnB", [128, NCH, 4], F32, st)
        P.dma(P.sp, sgn[:], io["sgnp"], writes=[cb])
        pw2 = P.sb("pw2", [128, NBIS], F32, st)
        for it in range(NBIS):
            P.op(P.pool, nc.gpsimd.memset, pw2[:, it:it + 1], 2.0 ** -(it + 1), writes=[cb])
        S = P.sb("S", [64, 768], F32, st)
        Sb = Buf("S")
        P.op(P.pool, nc.gpsimd.memset, S[:], 0.0, writes=[Sb])
        own = P.sb("own", [64, 768], F32, st)
        ownbf = P.sb("ownbf", [64, 768], BF16, st)
        tmpS = P.sb("tmpS", [64, 768], F32, st)
        ownb = Buf("own")
        kvl = P.sb("kvl", [64, 4, 768], F32, st)
        kvlb = Buf("kvl")
        qkT = P.sb("qkT", [64, 3, 6, 128], BF16, st)
        qkTb = Buf("qkT")
        vown = P.sb("vown", [128, 768], BF16, st)
        sgr = P.sb("sgrB", [128, 768], BF16, st)
        vob = Buf("vown")
        scbf = P.sb("scbf", [128, 768], BF16, st)
        scb = Buf("scbf")
        of = P.sb("of", [128, 768], F32, st)
        I = P.sb("I", [128, 16384], F32, st)
        Ib = Buf("I")
        of2 = I[:, 0:768]
        ofb = Buf("of")
        st8 = P.sb("st8", [128, 32], F32, st)
        obf = P.sb("obfB", [128, 768], BF16, st)
        obfb = Buf("obfB")
        oT = P.sb("oT", [128, 6, 128], BF16, st)
        oTb = Buf("oT")
        maskb = P.sb("maskb", [128, 16384], BF16, st)
        mkb = Buf("mask")
        maskT = P.sb("maskT", [128, 128, 128], BF16, st)
        mTb = Buf("maskT")
        qaT = P.sb("qaT", [128, 6, 128], BF16, st)
        qwT = P.sb("qwT", [128, 2, 128], BF16, st)
        qab = Buf("qaT")
        kiT = [P.sb("kiT%d" % i, [128, 512], BF16, st) for i in range(2)]
        kib = bufs(2, "kiT")
        rh = [P.sb("rh%d" % i, [128, 512], F32, st) for i in range(2)]
        rhb = bufs(2, "rh")
        KT = [P.sb("KT%d" % i, [128, 512], BF16, st) for i in range(2)]
        KTb = bufs(2, "KT")
        Vx = [P.sb("Vx%d" % i, [128, 4, 132], BF16, st) for i in range(2)]
        Vxb = bufs(2, "Vx")
        for i in range(2):
            P.op(P.pool, nc.gpsimd.memset, Vx[i][:], 1.0, writes=[Vxb[i]])
        eb = [P.sb("eb%d" % i, [128, 512], BF16, st) for i in range(2)]
        ebb = bufs(2, "eb")
        pb_ = [P.sb("pb%d" % i, [128, 512], BF16, st) for i in range(2)]
        pbb = bufs(2, "pb")
        tS = [P.sb("tS%d" % i, [128, 512], F32, st) for i in range(1)] * 2
        tSb = bufs(1, "tS") * 2
        bs = P.sb("bs", [128, 8 + NBIS], F32, st)
        bsb = Buf("bs")
        cnt_i = [0]

        for i in range(nch):
            cs = slice(i * 128, (i + 1) * 128)
            P.dma(P.sp, kvl[:], io["kv_all"][4 * i:4 * i + 4].rearrange("j d n -> d j n"), writes=[kvlb])
            v3 = lambda ap: ap.rearrange("p (h v) -> p h v", h=6)
            bc = lambda ap: ap.unsqueeze(2).to_broadcast([64, 6, 128])
            P.op(P.dve, V.tensor_tensor, v3(own[:]), v3(S[:]), bc(cA[:, 0, :]), ALU.mult, reads=[Sb, cb], writes=[ownb])
            for jj in range(4):
                P.op(P.dve, V.tensor_tensor, v3(tmpS[:]), v3(kvl[:, jj, :]), bc(cA[:, 1 + jj, :]), ALU.mult, reads=[kvlb, cb], writes=[ownb])
                P.op(P.dve, V.tensor_tensor, own[:], own[:], tmpS[:], ALU.add, reads=[ownb], writes=[ownb])
            P.op(P.dve, V.tensor_copy, ownbf[:], own[:], reads=[ownb], writes=[ownb])
            P.op(P.dve, V.tensor_tensor, v3(S[:]), v3(S[:]), bc(cU[:, 0, :]), ALU.mult, reads=[Sb, cb], writes=[Sb])
            for jj in range(4):
                P.op(P.dve, V.tensor_tensor, v3(tmpS[:]), v3(kvl[:, jj, :]), bc(cU[:, 1 + jj, :]), ALU.mult, reads=[kvlb, cb, ownb], writes=[ownb])
                P.op(P.dve, V.tensor_tensor, S[:], S[:], tmpS[:], ALU.add, reads=[ownb, Sb], writes=[Sb])
            P.dma(P.sp, qkT[:], io["qkT"][:, :, :, cs].rearrange("a h d t -> d a h t"), writes=[qkTb])
            P.dma(P.sp, vown[:], io["vr"][cs, :], writes=[vob])
            P.dma(P.sp, sgr[:], io["sgr"][cs, :], writes=[vob])
            pA, pAb = C.bank()
            pB, pBb = C.bank()
            for h in range(6):
                ps, pbk = (pA, pAb) if h < 4 else (pB, pBb)
                hh = h % 4
                P.op(P.pe, nc.tensor.matmul, ps[:, hh * 128:(hh + 1) * 128], qkT[:, 2, h, :], qkT[:, 0, h, :], start=True, stop=True,
                     reads=[qkTb], writes=[pbk])
            P.op(P.dve, V.tensor_tensor, scbf[:, 0:512], pA[:], dintra[:, 0:512], ALU.mult, reads=[pAb, cb], writes=[scb])
            P.op(P.dve, V.tensor_tensor, scbf[:, 512:768], pB[:, 0:256], dintra[:, 512:768], ALU.mult, reads=[pBb, cb], writes=[scb])
            oA, oAb = C.bank()
            oB, oBb = C.bank()
            for h in range(6):
                ps, pbk = (oA, oAb) if h < 4 else (oB, oBb)
                hh = h % 4
                P.op(P.pe, nc.tensor.matmul, ps[:, hh * 128:(hh + 1) * 128], scbf[:, h * 128:(h + 1) * 128], vown[:, h * 128:(h + 1) * 128],
                     start=True, stop=False, reads=[scb, vob], writes=[pbk], inc=False)
                P.op(P.pe, nc.tensor.matmul, ps[:, hh * 128:(hh + 1) * 128], qkT[:, 1, h, :], ownbf[:, h * 128:(h + 1) * 128],
                     start=False, stop=True, reads=[qkTb, ownb], writes=[pbk])
            P.copy(P.act, of[:, 0:512], oA[:], reads=[oAb], writes=[ofb])
            P.copy(P.act, of[:, 512:768], oB[:, 0:256], reads=[oBb], writes=[ofb])
            of3 = of[:].rearrange("p (h v) -> p h v", h=6)
            of23 = of2.rearrange("p (h v) -> p h v", h=6)
            rw = dict(reads=[ofb, Ib], writes=[ofb, Ib])
            mean, var, rstd = st8[:, 0:6], st8[:, 8:14], st8[:, 16:22]
            P.op(P.dve, V.tensor_reduce, out=mean, in_=of3, axis=AX.X, op=ALU.add, **rw)
            P.op(P.dve, V.tensor_scalar, mean, mean, 1.0 / 128, None, ALU.mult, **rw)
            P.op(P.dve, V.tensor_tensor, of3, of3, mean.unsqueeze(2).to_broadcast([128, 6, 128]), ALU.subtract, **rw)
            P.op(P.act, nc.scalar.activation, out=of2, in_=of[:], func=AF.Square, **rw)
            P.op(P.dve, V.tensor_reduce, out=var, in_=of23, axis=AX.X, op=ALU.add, **rw)
            P.op(P.act, nc.scalar.activation, out=var, in_=var, func=AF.Sqrt, scale=1.0 / 128, bias=C.epsb[:, 0:1], **rw)
            P.op(P.dve, V.reciprocal, rstd, var, **rw)
            P.op(P.dve, V.tensor_tensor, of3, of3, rstd.unsqueeze(2).to_broadcast([128, 6, 128]), ALU.mult, **rw)
            P.op(P.dve, V.tensor_tensor, of[:], of[:], gret[:], ALU.mult, reads=[ofb, cb], writes=[ofb])
            P.op(P.dve, V.tensor_tensor, obf[:], of[:], sgr[:], ALU.mult, reads=[ofb, vob], writes=[obfb])
            transpose_to(P, C, lambda k: obf[:, k * 128:(k + 1) * 128], 6, lambda k0, kn: oT[:, k0:k0 + kn, :], [obfb], [oTb])
            P.dma(P.sp, io["orT"][:, :, cs].rearrange("k f t -> f k t"), oT[:], reads=[oTb])

            ng = i + 1
            N = ng * 512
            P.dma(P.sp, qaT[:], io["qaT"][:, :, cs].rearrange("h d t -> d h t"), writes=[qab])
            P.dma(P.sp, qwT[:], io["qwT"][:, :, cs].rearrange("a d t -> d a t"), writes=[qab])
            for g in range(ng):
                kk = cnt_i[0] % 2
                cnt_i[0] += 1
                P.dma(P.sp, kiT[kk][:], io["kiT"][:, g * 512:(g + 1) * 512], writes=[kib[kk]])
                Ig = I[:, g * 512:(g + 1) * 512]
                for h in range(4):
                    ps, pbk = C.bank()
                    lo_ = (h % 2) * 64
                    P.op(P.pe, nc.tensor.matmul, ps[:], qwT[lo_:lo_ + 64, h // 2, :], kiT[kk][lo_:lo_ + 64, :], start=True, stop=True,
                         reads=[qab, kib[kk]], writes=[pbk])
                    r = (cnt_i[0] + h) % 2
                    P.op(P.act, nc.scalar.activation, out=rh[r][:], in_=ps[:], func=AF.Relu, reads=[pbk], writes=[rhb[r]])
                    if h == 0:
                        P.op(P.dve, V.tensor_scalar, Ig, rh[r][:], sgn[:, i, 0:1], None, ALU.mult, reads=[rhb[r], cb], writes=[Ib])
                    else:
                        P.op(P.dve, V.scalar_tensor_tensor, out=Ig, in0=rh[r][:], scalar=sgn[:, i, h:h + 1], in1=Ig, op0=ALU.mult, op1=ALU.add,
                             reads=[rhb[r], cb, Ib], writes=[Ib])
            lo, hi, w0, mid, cnt, dd = [bs[:, k:k + 1] for k in range(6)]
            hw = bs[:, 8:8 + NBIS]
            bw = dict(reads=[Ib, bsb], writes=[bsb])
            P.op(P.dve, V.tensor_reduce, out=lo, in_=I[:, 0:N], axis=AX.X, op=ALU.min, **bw)
            P.op(P.dve, V.tensor_reduce, out=hi, in_=I[:, 0:N], axis=AX.X, op=ALU.max, **bw)
            P.op(P.dve, V.tensor_scalar, lo, lo, -1.0, None, ALU.add, **bw)
            P.op(P.dve, V.tensor_tensor, w0, hi, lo, ALU.subtract, **bw)
            P.op(P.dve, V.tensor_scalar, hw, pw2[:], w0, None, ALU.mult, reads=[bsb, cb], writes=[bsb])
            P.op(P.dve, V.tensor_tensor, I[:, N - 512:N], I[:, N - 512:N], cmask[:], ALU.add, reads=[Ib, cb, bsb], writes=[Ib])
            for it in range(NBIS):
                P.op(P.dve, V.tensor_tensor, mid, lo, hw[:, it:it + 1], ALU.add, **bw)
                P.op(P.dve, V.tensor_scalar, maskb[:, 0:N], I[:, 0:N], mid, 0.0, ALU.is_gt, ALU.add, accum_out=cnt,
                     reads=[Ib, bsb], writes=[mkb, bsb])
                P.op(P.dve, V.tensor_scalar, dd, cnt, 255.5, hw[:, it:it + 1], ALU.is_gt, ALU.mult, **bw)
                P.op(P.dve, V.tensor_tensor, lo, lo, dd, ALU.add, **bw)
            P.op(P.dve, V.tensor_scalar, maskb[:, 0:N], I[:, 0:N], lo, None, ALU.is_gt, reads=[Ib, bsb], writes=[mkb])
            transpose_to(P, C, lambda k: maskb[:, k * 128:(k + 1) * 128], 4 * ng, lambda k0, kn: maskT[:, k0:k0 + kn, :], [mkb], [mTb])
            for h in range(6):
                oacc, oaccb = C.accbank()
                for g in range(ng):
                    kk = cnt_i[0] % 2
                    cnt_i[0] += 1
                    P.dma(P.sp, KT[kk][:], io["KaT"][h, :, g * 512:(g + 1) * 512], writes=[KTb[kk]])
                    P.dma(P.sp, Vx[kk][:, :, 0:128], io["Va"][g * 512:(g + 1) * 512, h * 128:(h + 1) * 128].rearrange("(k p) v -> p k v", p=128),
                          writes=[Vxb[kk]])
                    ps, pbk = C.bank()
                    for kb in range(4):
                        P.op(P.pe, nc.tensor.matmul, ps[:, kb * 128:(kb + 1) * 128], KT[kk][:, kb * 128:(kb + 1) * 128], qaT[:, h, :],
                             start=True, stop=True, reads=[KTb[kk], qab], writes=[pbk], inc=(kb == 3))
                    e2 = cnt_i[0] % 2
                    if g >= ng - 2:
                        r0 = (g - (ng - 2)) * 4
                        if ng == 1:
                            r0 = 4
                        P.op(P.dve, V.tensor_tensor, tS[e2][:].rearrange("p (k t) -> p k t", k=4), ps[:].rearrange("p (k t) -> p k t", k=4),
                             nearb[:, h, r0:r0 + 4, :], ALU.add, reads=[pbk, cb], writes=[tSb[e2]])
                        P.op(P.act, nc.scalar.activation, out=eb[e2][:], in_=tS[e2][:], func=AF.Exp, reads=[tSb[e2]], writes=[ebb[e2]])
                    else:
                        P.op(P.act, nc.scalar.activation, out=eb[e2][:], in_=ps[:], func=AF.Exp, bias=farb[:, h:h + 1],
                             reads=[pbk, cb], writes=[ebb[e2]])
                    P.op(P.dve, V.tensor_tensor, pb_[e2][:], eb[e2][:], maskT[:, g * 4:(g + 1) * 4, :].rearrange("p k t -> p (k t)"), ALU.mult,
                         reads=[ebb[e2], mTb], writes=[pbb[e2]])
                    for kb in range(4):
                        P.op(P.pe, nc.tensor.matmul, oacc[:, 0:129], pb_[e2][:, kb * 128:(kb + 1) * 128], Vx[kk][:, kb, 0:129],
                             start=(g == 0 and kb == 0), stop=(g == ng - 1 and kb == 3), reads=[pbb[e2], Vxb[kk]], writes=[oaccb],
                             inc=(kb == 3))
                rs = bs[:, 6:7]
                P.op(P.dve, V.reciprocal, rs, oacc[:, 128:129], reads=[oaccb, bsb], writes=[bsb])
                P.op(P.dve, V.tensor_scalar, obf[:, h * 128:(h + 1) * 128], oacc[:, 0:128], rs, None, ALU.mult,
                     reads=[oaccb, bsb], writes=[obfb])
            transpose_to(P, C, lambda k: obf[:, k * 128:(k + 1) * 128], 6, lambda k0, kn: oT[:, k0:k0 + kn, :], [obfb], [oTb])
            P.dma(P.sp, io["oaT"][:, :, cs].rearrange("k f t -> f k t"), oT[:], reads=[oTb])
        P.barrier()
    C.mm_n = 6


def phase_b2(P, C, l, io, last):
    nc = P.nc
    V = nc.vector
    with ExitStack() as st:
        W = Work(P, C, st)
        C.alloc_slots(st)
        gt = [P.sb("gt%d" % i, [128, 4, 512], BF16, st) for i in range(2)]
        gtb = bufs(2, "gt")
        macc = P.sb("macc", [128, 4, 512], F32, st)
        mtmp = P.sb("mtmp", [128, 512], F32, st)
        mbf = P.sb("mbf", [128, 4, 512], BF16, st)
        maccb = bufs(4, "macc")
        mbfb = bufs(4, "mbf")
        gi = [0]
        brs = [("orT", 6, "w_br_ret", 0), ("oaT", 6, "w_br_att", 1), ("osT", 4, "w_br_sg", 2)]
        for tt in range(TT):
            rows = slice(tt * 512, (tt + 1) * 512)
            P.dma(P.sp, W.xt[:], io["x1"][rows, :].rearrange("(c p) d -> p c d", p=128), writes=W.xtb)
            aT = W.hT
            P.dma(P.sp, aT[:, 0:6, :], io["orT"][:, :, rows].rearrange("k f t -> f k t"), writes=W.hTb[0:6])
            P.dma(P.sp, aT[:, 6:12, :], io["oaT"][:, :, rows].rearrange("k f t -> f k t"), writes=W.hTb[6:12])
            P.dma(P.sp, aT[:, 12:16, :], io["osT"][:, :, rows].rearrange("k f t -> f k t"), writes=W.hTb[12:16])
            for nb in range(4):
                for (nm, K, wn, bi) in brs:
                    k0 = {0: 0, 1: 6, 2: 12}[bi]
                    g = gi[0] % 2
                    gi[0] += 1
                    P.dma(P.sp, gt[g][:], io["gates"][rows, bi * 2048 + nb * 512: bi * 2048 + (nb + 1) * 512].rearrange("(c p) n -> p c n", p=128),
                          writes=[gtb[g]])

                    def evac(c, ps, pb, bi=bi, g=g):
                        if bi == 0:
                            P.op(P.dve, V.tensor_tensor, macc[:, c, :], ps, gt[g][:, c, :], ALU.mult, reads=[pb, gtb[g]], writes=[maccb[c]])
                        else:
                            P.op(P.dve, V.tensor_tensor, mtmp[:], ps, gt[g][:, c, :], ALU.mult, reads=[pb, gtb[g]], writes=[maccb[c]])
                            P.op(P.dve, V.tensor_tensor, macc[:, c, :], macc[:, c, :], mtmp[:], ALU.add, reads=[maccb[c]], writes=[maccb[c]])
                    linear_tok(P, C, lambda k, c, k0=k0: aT[:, k0 + k, c * 128:(c + 1) * 128], W.hTb[k0:k0 + K], K, io[wn], nb * 512, 512, evac)
                for c in range(4):
                    P.copy(P.act, mbf[:, c, :], macc[:, c, :], reads=[maccb[c]], writes=[mbfb[c]])
                    transpose_to(P, C, lambda k, c=c: mbf[:, c, k * 128:(k + 1) * 128], 4,
                                 lambda k0, kn, c=c, nb=nb: W.xT[:, nb * 4 + k0:nb * 4 + k0 + kn, c * 128:(c + 1) * 128], [mbfb[c]], [W.xTb])
            for nb in range(4):
                def evac(c, ps, pb, nb=nb):
                    xs = W.xt[:, c, nb * 512:(nb + 1) * 512]
                    P.op(P.dve, V.tensor_tensor, xs, xs, ps, ALU.add, reads=[pb, W.xtb[c]], writes=[W.xtb[c]])
                linear_tok(P, C, lambda k, c: W.xT[:, k, c * 128:(c + 1) * 128], [W.xTb], 16, io["w_out"], nb * 512, 512, evac)
            ffn(P, C, W, io["n2"], io["wg2"], io["wu2"], io["wd2"])
            if last:
                P.dma(P.sp, W.gbc[:], io["nfin"].partition_broadcast(128), writes=[W.gbcb])
                for c in range(4):
                    ss, sd, rs = W.st[:, c:c + 1], W.st[:, 4 + c:5 + c], W.st[:, 8 + c:9 + c]
                    P.op(P.act, nc.scalar.activation, out=W.junk[:], in_=W.xt[:, c, :], func=AF.Square, accum_out=ss,
                         reads=[W.xtb[c]], writes=[W.junkb, W.stb])
                    P.op(P.act, nc.scalar.activation, out=sd, in_=ss, func=AF.Sqrt, scale=1.0 / D, bias=C.epsb[:, 0:1],
                         reads=[W.stb], writes=[W.stb])
                    P.op(P.dve, V.reciprocal, rs, sd, reads=[W.stb], writes=[W.stb])
                    P.op(P.dve, V.scalar_tensor_tensor, out=W.xt[:, c, :], in0=W.xt[:, c, :], scalar=rs, in1=W.gbc[:],
                         op0=ALU.mult, op1=ALU.mult, reads=[W.xtb[c], W.stb, W.gbcb], writes=[W.xtb[c]])
            P.dma(P.sp, io["xout"][rows, :].rearrange("(c p) d -> p c d", p=128), W.xt[:], reads=W.xtb)
        P.barrier()


B_INS = [("x1", [4096, D], F32), ("qkT", [3, 6, 64, 4096], BF16), ("vr", [4096, 768], BF16), ("sgr", [4096, 768], BF16),
         ("gates", [4096, 6144], BF16), ("osT", [4, 128, 4096], BF16), ("qaT", [6, 128, 4096], BF16), ("qwT", [2, 128, 4096], BF16),
         ("sgnp", [128, NCH, 4], F32), ("KaT", [6, 128, 16384], BF16), ("Va", [16384, 768], BF16), ("kiT", [128, 16384], BF16),
         ("kv_all", [128, 64, 768], F32), ("dintraT", [128, 768], F32), ("cA", [64, 5, 6], F32), ("cU", [64, 5, 6], F32),
         ("ret_norm", [768], F32), ("cmask", [128, 512], F32), ("nearbias", [128, 6, 8, 128], BF16), ("farbias", [128, 6], F32),
         ("w_br_ret", [768, D], F32), ("w_br_att", [768, D], F32), ("w_br_sg", [512, D], F32), ("w_out", [D, D], F32),
         ("n2", [D], F32), ("wg2", [D, DFF], F32), ("wu2", [D, DFF], F32), ("wd2", [DFF, D], F32), ("nfin", [D], F32)]


def build_b(last, nch=NCH, do_b2=True):
    P = Prog()
    C = Common(P)
    add_consts(P, C)
    io = declare(P, B_INS, "ExternalInput")
    io["xout"] = P.dram("xout", [4096, D], F32, "ExternalOutput")
    io["orT"] = P.dram("orT", [6, 128, 4096], BF16, "ExternalOutput")
    io["oaT"] = P.dram("oaT", [6, 128, 4096], BF16, "ExternalOutput")
    phase_b1(P, C, 0, io, nch)
    if do_b2:
        phase_b2(P, C, 0, io, last)
    return P


def to_global(parts):
    s = parts[0].shape[1:]
    g = np.empty((128, 128) + s, dtype=parts[0].dtype)
    for j in range(4):
        g[j::4] = parts[j].reshape((32, 128) + s)
    return g.reshape((16384,) + s)


def t5_bucket_np(dist):
    dist = np.maximum(dist, 0)
    df = np.maximum(dist, 1).astype(np.float32)
    large = 16 + (np.log(df / np.float32(16)) / np.float32(math.log(128 / 16)) * np.float32(16)).astype(np.int32)
    large = np.minimum(large, 31)
    return np.where(dist < 16, dist, large)


def b_consts(inp, l, j):
    rc = ret_consts()
    cd = rc["cd"]
    cA = np.zeros((64, 5, 6), np.float32)
    cU = np.zeros((64, 5, 6), np.float32)
    cA[:, 0, :] = cd ** j
    cU[:, 0, :] = cd ** 4
    for jp in range(4):
        if jp < j:
            cA[:, 1 + jp, :] = cd ** (j - 1 - jp)
        cU[:, 1 + jp, :] = cd ** (3 - jp)
    t = np.arange(128)[:, None]
    sp = np.arange(512)[None, :]
    jp, sin = sp // 128, sp % 128
    vis = (jp < j) | ((jp == j) & (sin <= t))
    cmask = np.where(vis, 0.0, NEG).astype(np.float32)
    s_ = np.arange(128)[:, None, None]
    r_ = np.arange(8)[None, :, None]
    t_ = np.arange(128)[None, None, :]
    dist = (4 + j - r_) * 128 + t_ - s_
    bk = t5_bucket_np(dist)
    tab = inp["rel_table"]
    nb = tab[bk]
    nb = np.where((dist >= 0)[..., None], nb, np.float32(0.0))
    nearbias = np.ascontiguousarray(nb.transpose(0, 3, 1, 2)).astype(ml_dtypes.bfloat16)
    farbias = np.ascontiguousarray(np.broadcast_to(tab[31][None, :], (128, 6))).astype(np.float32)
    return dict(cA=cA, cU=cU, cmask=cmask, nearbias=nearbias, farbias=farbias,
                dintraT=np.ascontiguousarray(rc["dintra"].transpose(2, 0, 1).reshape(128, 768)))


def b_inputs(inp, l, c, aout):
    b, j = c // 4, c % 4
    grp = [aout[4 * b + jj] for jj in range(4)]
    me = aout[c]
    T = lambda a: np.ascontiguousarray(a)
    qkr = me["qkr"].reshape(4096, 4, 6, 64)
    ki = to_global([g["qwki"][:, 256:320] for g in grp]).T
    kv = np.empty((128, 6, 64, 128), np.float32)
    for jj in range(4):
        kv[jj::4] = grp[jj]["kv"]
    m = {
        "x1": me["x1"], "qkT": T(qkr[:, [0, 1, 2]].transpose(1, 2, 3, 0)), "vr": me["vr"], "sgr": me["sgr"], "gates": me["gates"],
        "osT": T(me["os"].reshape(4096, 4, 128).transpose(1, 2, 0)),
        "qaT": T(me["qa"].reshape(4096, 6, 128).transpose(1, 2, 0)),
        "qwT": T(me["qwki"][:, 0:256].reshape(4096, 2, 128).transpose(1, 2, 0)),
        "sgnp": T(me["sgn"].reshape(32, 128, 4).transpose(1, 0, 2)),
        "KaT": T(to_global([g["ka"] for g in grp]).reshape(16384, 6, 128).transpose(1, 2, 0)),
        "Va": to_global([g["va"] for g in grp]),
        "kiT": T(np.concatenate([ki, ki], axis=0)),
        "kv_all": T(kv.transpose(0, 2, 1, 3).reshape(128, 64, 768)),
        "ret_norm": inp["ret_norm"][l],
        "w_br_ret": inp["w_br_ret"][l], "w_br_att": inp["w_br_att"][l], "w_br_sg": inp["w_br_sg"][l], "w_out": inp["w_out"][l],
        "n2": inp["ffn2_norm"][l], "wg2": inp["ffn2_w_gate"][l], "wu2": inp["ffn2_w_up"][l], "wd2": inp["ffn2_w_down"][l],
        "nfin": inp["final_norm"],
    }
    m.update(b_consts(inp, l, j))
    return m


_PROGS = {}


def kernel(**inp):
    inp = {k: np.asarray(v) for k, v in inp.items()}
    cores = list(range(8))
    xcur = {c: own_rows(inp["x"][c // 4], c % 4) for c in cores}
    for l in range(2):
        PA = build_a()
        ra = run_bass_kernel_spmd(PA.nc, [a_inputs(inp, l, c, xcur) for c in cores], core_ids=cores)
        aout = ra.results
        PB = build_b(last=(l == 1))
        rb = run_bass_kernel_spmd(PB.nc, [b_inputs(inp, l, c, aout) for c in cores], core_ids=cores)
        xcur = {c: rb.results[c]["xout"] for c in cores}
    out = np.empty((2, 16384, D), np.float32)
    for b in range(2):
        out[b] = to_global([xcur[4 * b + j] for j in range(4)])
    return out
```
